# Optimizing a Trainium2 kernel written in Bass

```python
import math
import jax, jax.numpy as jnp
from jax import lax
import numpy as np

D_MODEL = 2048
BATCH = 2
SEQ = 8192
DEPTH = 4

N_MIXERS = 3
N_MLA = (DEPTH + 2) // 3
N_RWKV = (DEPTH + 1) // 3
N_SWA = DEPTH // 3

DEEPNORM_ALPHA = (2.0 * DEPTH) ** 0.25
DEEPNORM_BETA = (8.0 * DEPTH) ** -0.25
LN_EPS = 1e-5
RMS_EPS = 1e-6
NEG_INF = -1e30
Q_BLOCK = 128

MLA_HEADS = 16
MLA_Q_RANK = 512
MLA_KV_RANK = 512
MLA_NOPE = 128
MLA_ROPE = 64
MLA_V = 128
ROPE_THETA = 10000.0

RWKV_HEAD = 64
RWKV_HEADS = D_MODEL // RWKV_HEAD
RWKV_DECAY_LORA = 96
RWKV_AAA_LORA = 96
RWKV_GATE_LORA = 256
RWKV_GN_EPS = 64e-5

SWA_HEAD = 64
SWA_Q_HEADS = D_MODEL // SWA_HEAD
SWA_KV_HEADS = 4
SWA_GROUP = SWA_Q_HEADS // SWA_KV_HEADS
WINDOW = 128
REL_BUCKETS = 32
REL_MAX_DIST = WINDOW

MLP_HIDDEN = 4 * D_MODEL

kernel_name = "hybrid_mla_rwkv7_swa_deepnorm"


def layer_norm(x, g, b):
    xf = x.astype(jnp.float32)
    mu = jnp.mean(xf, -1, keepdims=True)
    var = jnp.mean(jnp.square(xf - mu), -1, keepdims=True)
    return ((xf - mu) * lax.rsqrt(var + LN_EPS) * g + b).astype(x.dtype)


def rms_norm(x, g):
    xf = x.astype(jnp.float32)
    return (xf * lax.rsqrt(jnp.mean(jnp.square(xf), -1, keepdims=True) + RMS_EPS) * g).astype(x.dtype)


def rope(x, pos):
    half = x.shape[-1] // 2
    inv = ROPE_THETA ** (-jnp.arange(half, dtype=jnp.float32) / half)
    ang = pos.astype(jnp.float32)[:, None] * inv[None, :]
    cos = jnp.cos(ang)[:, None, :]
    sin = jnp.sin(ang)[:, None, :]
    x1 = x[..., :half].astype(jnp.float32)
    x2 = x[..., half:].astype(jnp.float32)
    return jnp.concatenate([x1 * cos - x2 * sin, x2 * cos + x1 * sin], -1).astype(x.dtype)


def causal_block_attention(q, k, v, scale):
    B, S, H, Dq = q.shape
    nb = S // Q_BLOCK
    q_blocks = q.reshape(B, nb, Q_BLOCK, H, Dq).swapaxes(0, 1)
    k_pos = jnp.arange(S)

    def one_block(args):
        qb, start = args
        s = jnp.einsum('bqhd,bkhd->bhqk', qb, k).astype(jnp.float32) * scale
        q_pos = start + jnp.arange(Q_BLOCK)
        s = jnp.where(k_pos[None, :] <= q_pos[:, None], s, NEG_INF)
        p = jax.nn.softmax(s, axis=-1).astype(v.dtype)
        return jnp.einsum('bhqk,bkhd->bqhd', p, v)

    o = lax.map(one_block, (q_blocks, jnp.arange(nb) * Q_BLOCK))
    return o.swapaxes(0, 1).reshape(B, S, H, v.shape[-1])


def mla_mixer(x, w_in, q_norm, kv_norm, w_q_b, w_kv_b, w_out):
    B, S, _ = x.shape
    h = x @ w_in
    q_lat, kv_lat, k_rope = jnp.split(h, [MLA_Q_RANK, MLA_Q_RANK + MLA_KV_RANK], axis=-1)
    q = (rms_norm(q_lat, q_norm) @ w_q_b).reshape(B, S, MLA_HEADS, MLA_NOPE + MLA_ROPE)
    kv = (rms_norm(kv_lat, kv_norm) @ w_kv_b).reshape(B, S, MLA_HEADS, MLA_NOPE + MLA_V)
    pos = jnp.arange(S)
    q = jnp.concatenate([q[..., :MLA_NOPE], rope(q[..., MLA_NOPE:], pos)], -1)
    k_rope = rope(k_rope[:, :, None, :], pos)
    k = jnp.concatenate([kv[..., :MLA_NOPE],
                         jnp.broadcast_to(k_rope, (B, S, MLA_HEADS, MLA_ROPE))], -1)
    v = kv[..., MLA_NOPE:]
    o = causal_block_attention(q, k, v, (MLA_NOPE + MLA_ROPE) ** -0.5)
    return o.reshape(B, S, MLA_HEADS * MLA_V) @ w_out


def wkv7_scan(r, w, k, v, a, b):
    B, S, H, N = r.shape
    seq_first = lambda t: jnp.moveaxis(t.astype(jnp.float32), 1, 0)

    def step(state, inp):
        rt, wt, kt, vt, at, bt = inp
        sa = jnp.einsum('bhvk,bhk->bhv', state, at)
        state = (state * wt[:, :, None, :] + sa[..., None] * bt[:, :, None, :]
                 + vt[..., None] * kt[:, :, None, :])
        return state, jnp.einsum('bhvk,bhk->bhv', state, rt)

    state0 = jnp.zeros((B, H, N, N), jnp.float32)
    _, y = lax.scan(step, state0, tuple(seq_first(t) for t in (r, w, k, v, a, b)))
    return jnp.moveaxis(y, 0, 1)


def rwkv7_mixer(x, mix, w_in, w0, w1, w2, a0, a1, a2, g1, g2, k_k, k_a, r_k, ln_g, ln_b, w_out):
    B, S, D = x.shape
    H, N = RWKV_HEADS, RWKV_HEAD
    heads = lambda t: t.reshape(B, S, H, N)
    xx = jnp.pad(x, ((0, 0), (1, 0), (0, 0)))[:, :-1] - x
    x_rkv = x[None] + xx[None] * mix[jnp.array([0, 2, 3])][:, None, None, :]
    xw = x + xx * mix[1]
    xa = x + xx * mix[4]
    xg = x + xx * mix[5]
    r, k, v = jnp.einsum('nbsd,nde->nbse', x_rkv, w_in)
    log_w = -jax.nn.softplus(-(w0 + jnp.tanh(xw @ w1) @ w2)) - 0.5
    decay = jnp.exp(-jnp.exp(log_w.astype(jnp.float32)))
    a = jax.nn.sigmoid(a0 + (xa @ a1) @ a2)
    g = jax.nn.sigmoid(xg @ g1) @ g2
    kk = heads(k * k_k).astype(jnp.float32)
    kk = kk / jnp.maximum(jnp.sqrt(jnp.sum(jnp.square(kk), -1, keepdims=True)), 1e-12)
    k = k * (1.0 + (a - 1.0) * k_a)
    y = wkv7_scan(heads(r), heads(decay), heads(k), heads(v), -kk, kk * heads(a).astype(jnp.float32))
    mu = jnp.mean(y, -1, keepdims=True)
    var = jnp.mean(jnp.square(y - mu), -1, keepdims=True)
    y = ((y - mu) * lax.rsqrt(var + RWKV_GN_EPS)).reshape(B, S, D) * ln_g + ln_b
    bonus = jnp.sum(heads(r) * heads(k) * r_k, -1, keepdims=True) * heads(v)
    y = y + bonus.reshape(B, S, D)
    return (y * g).astype(x.dtype) @ w_out


def t5_bucket(dist):
    max_exact = REL_BUCKETS // 2
    n = jnp.maximum(dist, 0)
    nf = jnp.maximum(n, 1).astype(jnp.float32)
    large = max_exact + (jnp.log(nf / max_exact) / math.log(REL_MAX_DIST / max_exact)
                         * (REL_BUCKETS - max_exact)).astype(jnp.int32)
    large = jnp.minimum(large, REL_BUCKETS - 1)
    return jnp.where(n < max_exact, n, large)


def swa_mixer(x, w_in, b_in, sinks, w_out, b_out, rel_bias):
    B, S, _ = x.shape
    Hq, Hk, G, Dh, W = SWA_Q_HEADS, SWA_KV_HEADS, SWA_GROUP, SWA_HEAD, WINDOW
    nb = S // W
    qkv = x @ w_in + b_in
    q, k, v = jnp.split(qkv, [Hq * Dh, (Hq + Hk) * Dh], axis=-1)
    q = q.reshape(B, nb, W, Hk, G, Dh)
    k = k.reshape(B, nb, W, Hk, Dh)
    v = v.reshape(B, nb, W, Hk, Dh)
    band = lambda t: jnp.concatenate(
        [jnp.pad(t, ((0, 0), (1, 0), (0, 0), (0, 0), (0, 0)))[:, :-1], t], axis=2)
    k_band, v_band = band(k), band(v)
    dist = jnp.arange(W)[:, None] + W - jnp.arange(2 * W)[None, :]
    in_window = (dist >= 0) & (dist < W)
    bias = rel_bias[t5_bucket(dist)].astype(jnp.float32)
    bias = jnp.where(in_window[..., None], bias, NEG_INF)
    bias = bias.transpose(2, 0, 1).reshape(Hk, G, W, 2 * W)
    sink = sinks.astype(jnp.float32).reshape(Hk, G, 1, 1)
    scale = Dh ** -0.5

    def one_block(args):
        qb, kb, vb, blk = args
        s = jnp.einsum('bqhgd,bshd->bhgqs', qb, kb).astype(jnp.float32) * scale + bias
        key_ok = blk * W + jnp.arange(2 * W) - W >= 0
        s = jnp.where(key_ok, s, NEG_INF)
        s = jnp.concatenate([s, jnp.broadcast_to(sink, s.shape[:-1] + (1,))], -1)
        p = jax.nn.softmax(s, axis=-1)[..., :-1].astype(vb.dtype)
        return jnp.einsum('bhgqs,bshd->bqhgd', p, vb)

    o = lax.map(one_block, (q.swapaxes(0, 1), k_band.swapaxes(0, 1), v_band.swapaxes(0, 1), jnp.arange(nb)))
    o = o.swapaxes(0, 1).reshape(B, S, Hq * Dh)
    return o @ w_out + b_out


def setup_inputs(seed: int = 0) -> dict:
    keys = iter(jax.random.split(jax.random.key(seed), 48))
    nrm = lambda shape, scale: jax.random.normal(next(keys), shape, jnp.float32) * scale
    D = D_MODEL
    mla_in = MLA_Q_RANK + MLA_KV_RANK + MLA_ROPE
    swa_in = (SWA_Q_HEADS + 2 * SWA_KV_HEADS) * SWA_HEAD
    return {
        "x": nrm((BATCH, SEQ, D), 1.0),
        "mla_w_in": nrm((N_MLA, D, mla_in), D ** -0.5),
        "mla_q_norm": 1.0 + nrm((N_MLA, MLA_Q_RANK), 0.02),
        "mla_kv_norm": 1.0 + nrm((N_MLA, MLA_KV_RANK), 0.02),
        "mla_w_q_b": nrm((N_MLA, MLA_Q_RANK, MLA_HEADS * (MLA_NOPE + MLA_ROPE)), MLA_Q_RANK ** -0.5),
        "mla_w_kv_b": nrm((N_MLA, MLA_KV_RANK, MLA_HEADS * (MLA_NOPE + MLA_V)), MLA_KV_RANK ** -0.5),
        "mla_w_out": nrm((N_MLA, MLA_HEADS * MLA_V, D), DEEPNORM_BETA * (MLA_HEADS * MLA_V) ** -0.5),
        "rwkv_mix": jax.random.uniform(next(keys), (N_RWKV, 6, D), jnp.float32),
        "rwkv_w_in": nrm((N_RWKV, 3, D, D), D ** -0.5),
        "rwkv_w0": jax.random.uniform(next(keys), (N_RWKV, D), jnp.float32, -6.0, -1.0),
        "rwkv_w1": nrm((N_RWKV, D, RWKV_DECAY_LORA), D ** -0.5),
        "rwkv_w2": nrm((N_RWKV, RWKV_DECAY_LORA, D), 0.5 * RWKV_DECAY_LORA ** -0.5),
        "rwkv_a0": nrm((N_RWKV, D), 0.02),
        "rwkv_a1": nrm((N_RWKV, D, RWKV_AAA_LORA), D ** -0.5),
        "rwkv_a2": nrm((N_RWKV, RWKV_AAA_LORA, D), 0.1 * RWKV_AAA_LORA ** -0.5),
        "rwkv_g1": nrm((N_RWKV, D, RWKV_GATE_LORA), D ** -0.5),
        "rwkv_g2": nrm((N_RWKV, RWKV_GATE_LORA, D), RWKV_GATE_LORA ** -0.5),
        "rwkv_k_k": 0.85 + nrm((N_RWKV, D), 0.02),
        "rwkv_k_a": 1.0 + nrm((N_RWKV, D), 0.02),
        "rwkv_r_k": nrm((N_RWKV, RWKV_HEADS, RWKV_HEAD), 0.1),
        "rwkv_ln_g": 1.0 + nrm((N_RWKV, D), 0.02),
        "rwkv_ln_b": nrm((N_RWKV, D), 0.02),
        "rwkv_w_out": nrm((N_RWKV, D, D), DEEPNORM_BETA * D ** -0.5),
        "swa_w_in": nrm((N_SWA, D, swa_in), D ** -0.5),
        "swa_b_in": nrm((N_SWA, swa_in), 0.02),
        "swa_sinks": nrm((N_SWA, SWA_Q_HEADS), 1.0),
        "swa_w_out": nrm((N_SWA, SWA_Q_HEADS * SWA_HEAD, D), DEEPNORM_BETA * (SWA_Q_HEADS * SWA_HEAD) ** -0.5),
        "swa_b_out": nrm((N_SWA, D), 0.02),
        "rel_bias": nrm((REL_BUCKETS, SWA_Q_HEADS), 0.5),
        "ln_g": 1.0 + nrm((DEPTH, 2, D), 0.02),
        "ln_b": nrm((DEPTH, 2, D), 0.02),
        "mlp_up": nrm((DEPTH, D, MLP_HIDDEN), D ** -0.5),
        "mlp_down": nrm((DEPTH, MLP_HIDDEN, D), DEEPNORM_BETA * MLP_HIDDEN ** -0.5),
    }


def reference(x, mla_w_in, mla_q_norm, mla_kv_norm, mla_w_q_b, mla_w_kv_b, mla_w_out,
              rwkv_mix, rwkv_w_in, rwkv_w0, rwkv_w1, rwkv_w2, rwkv_a0, rwkv_a1, rwkv_a2,
              rwkv_g1, rwkv_g2, rwkv_k_k, rwkv_k_a, rwkv_r_k, rwkv_ln_g, rwkv_ln_b, rwkv_w_out,
              swa_w_in, swa_b_in, swa_sinks, swa_w_out, swa_b_out,
              rel_bias, ln_g, ln_b, mlp_up, mlp_down):
    h = x
    for i in range(DEPTH):
        j = i // N_MIXERS
        kind = i % N_MIXERS
        if kind == 0:
            mixed = mla_mixer(h, mla_w_in[j], mla_q_norm[j], mla_kv_norm[j],
                              mla_w_q_b[j], mla_w_kv_b[j], mla_w_out[j])
        elif kind == 1:
            mixed = rwkv7_mixer(h, rwkv_mix[j], rwkv_w_in[j], rwkv_w0[j], rwkv_w1[j], rwkv_w2[j],
                                rwkv_a0[j], rwkv_a1[j], rwkv_a2[j], rwkv_g1[j], rwkv_g2[j],
                                rwkv_k_k[j], rwkv_k_a[j], rwkv_r_k[j], rwkv_ln_g[j], rwkv_ln_b[j],
                                rwkv_w_out[j])
        else:
            mixed = swa_mixer(h, swa_w_in[j], swa_b_in[j], swa_sinks[j], swa_w_out[j],
                              swa_b_out[j], rel_bias)
        h = layer_norm(DEEPNORM_ALPHA * h + mixed, ln_g[i, 0], ln_b[i, 0])
        ff = jnp.square(jax.nn.relu(h @ mlp_up[i])) @ mlp_down[i]
        h = layer_norm(DEEPNORM_ALPHA * h + ff, ln_g[i, 1], ln_b[i, 1])
    return h
```

```python
import math
from contextlib import ExitStack
import numpy as np
import concourse.bass as bass
import concourse.mybir as mybir
from concourse.bass_utils import run_bass_kernel_spmd

F32 = mybir.dt.float32
BF16 = mybir.dt.bfloat16
ALU = mybir.AluOpType
AF = mybir.ActivationFunctionType
AX = mybir.AxisListType

D = 2048
DEPTH = 4
ALPHA = (2.0 * DEPTH) ** 0.25
LN_EPS = 1e-5
RMS_EPS = 1e-6
HID = 8192
G4 = [[0, 1, 2, 3], [4, 5, 6, 7]]
G8 = [[0, 1, 2, 3, 4, 5, 6, 7]]

ENGS = ['pe', 'act', 'dve', 'pool', 'sp']
NSLOT = {'sp': 8, 'pool': 8, 'act': 4}


class Prog:
    def __init__(self, nc):
        self.nc = nc
        self.ops = {e: [] for e in ENGS}
        self.cnt = {}
        self.lastw = {}
        self.readers = {}
        self.seen = {e: {} for e in ENGS}
        self.dslot = {e: 0 for e in ENGS}
        self.stack = ExitStack()
        self.scopes = []
        self.sems = {}
        self.nt = 0

    def push(self):
        self.scopes.append(ExitStack())

    def pop(self):
        self.barrier()
        self.scopes.pop().close()

    def sb(self, shape, dt=F32):
        self.nt += 1
        st = self.scopes[-1] if self.scopes else self.stack
        return st.enter_context(self.nc.sbuf_tensor(f"sb{self.nt}", list(shape), dt))

    def ps(self, shape, dt=F32):
        self.nt += 1
        return self.stack.enter_context(self.nc.psum_tensor(f"ps{self.nt}", list(shape), dt))

    def _need(self, eng, waits, dep):
        k, v = dep
        if k == 'pe' and eng == 'pe':
            return
        if self.seen[eng].get(k, 0) >= v:
            return
        waits[k] = max(waits.get(k, 0), v)

    def op(self, eng, fn, reads=(), writes=(), dma=False, step=None, noinc=False):
        waits = {}
        for r in reads:
            if r in self.lastw:
                self._need(eng, waits, self.lastw[r])
        for w in writes:
            if w in self.lastw:
                self._need(eng, waits, self.lastw[w])
            for dep in self.readers.get(w, {}).items():
                self._need(eng, waits, dep)
        if dma:
            s = self.dslot[eng]
            self.dslot[eng] = (s + 1) % NSLOT[eng]
            semkey = ('dma', eng, s)
            st = 16 if step is None else step
            prev = self.cnt.get(semkey, 0)
            if prev > 0:
                self._need(eng, waits, (semkey, prev))
        else:
            semkey = eng
            st = 1
        for k, v in waits.items():
            self.seen[eng][k] = v
        if noinc:
            myval = self.cnt.get(semkey, 0) + st
            self.ops[eng].append((fn, list(waits.items()), None, 0))
        else:
            self.cnt[semkey] = self.cnt.get(semkey, 0) + st
            myval = self.cnt[semkey]
            self.ops[eng].append((fn, list(waits.items()), semkey, st))
        for w in writes:
            self.lastw[w] = (semkey, myval)
            self.readers[w] = {}
        for r in reads:
            d = self.readers.setdefault(r, {})
            d[semkey] = max(d.get(semkey, 0), myval)
        return myval

    def barrier(self):
        for e in ENGS:
            waits = {}
            for k, v in self.cnt.items():
                if k == e:
                    continue
                self._need(e, waits, (k, v))
            if waits:
                for k, v in waits.items():
                    self.seen[e][k] = v
                self.ops[e].append((None, list(waits.items()), None, 0))

    def emit(self):
        nc = self.nc
        for k in self.cnt:
            nm = k if isinstance(k, str) else "_".join(str(x) for x in k)
            self.sems[k] = self.stack.enter_context(nc.semaphore("s_" + nm))
        self.barrier()
        with nc.Block() as block:
            def mk(e):
                def body(eng):
                    for fn, waits, semkey, st in self.ops[e]:
                        for k, v in waits:
                            eng.wait_ge(self.sems[k], v)
                        if fn is not None:
                            ins = fn(eng)
                            if semkey is not None:
                                ins.then_inc(self.sems[semkey], st)
                return body
            block.tensor(mk('pe'))
            block.scalar(mk('act'))
            block.vector(mk('dve'))
            block.gpsimd(mk('pool'))
            block.sync(mk('sp'))
        self.stack.close()

    def dma(self, out, in_, reads=(), writes=(), eng='sp'):
        return self.op(eng, lambda e: e.dma_start(out=out, in_=in_), reads, writes, dma=True)

    def mm(self, out, lhsT, rhs, start=True, stop=True, reads=(), writes=(), noinc=False):
        return self.op('pe', lambda e: e.matmul(out, lhsT, rhs, start=start, stop=stop), reads, writes, noinc=noinc)

    def mmacc(self, out, pairs, reads=(), writes=()):
        n = len(pairs)
        for i, (l, r) in enumerate(pairs):
            self.mm(out, l, r, start=(i == 0), stop=(i == n - 1), reads=reads, writes=writes, noinc=(i < n - 1))

    def act(self, out, in_, func, reads=(), writes=(), **kw):
        return self.op('act', lambda e: e.activation(out, in_, func, **kw), reads, writes)

    def tt(self, out, a, b, op, reads=(), writes=(), eng='dve'):
        return self.op(eng, lambda e: e.tensor_tensor(out, a, b, op), reads, writes)

    def stt(self, out, in0, scalar, in1, op0, op1, reads=(), writes=(), eng='dve'):
        return self.op(eng, lambda e: e.scalar_tensor_tensor(out, in0, scalar, in1, op0, op1), reads, writes)

    def ts(self, out, in0, s1, s2, op0, op1=None, reads=(), writes=(), eng='dve'):
        if op1 is None:
            return self.op(eng, lambda e: e.tensor_scalar(out, in0, s1, None, op0), reads, writes)
        return self.op(eng, lambda e: e.tensor_scalar(out, in0, s1, s2, op0, op1), reads, writes)

    def cc(self, kind, aluop, groups, in_ap, out_ap, reads=(), writes=()):
        return self.op('pool', lambda e: e.collective_compute(kind, aluop, replica_groups=groups, ins=[in_ap], outs=[out_ap]),
                       reads, writes, dma=True, step=1)


class Banks:
    def __init__(self, ps_all, ids):
        self.ps = ps_all
        self.ids = list(ids)
        self.k = 0

    def next(self):
        i = self.ids[self.k % len(self.ids)]
        self.k += 1
        return self.ps[:, i, :], ('ps', i)


def lay(W):
    K, M = W.shape
    return np.ascontiguousarray(W.reshape(K // 128, 128, M // 128, 128).transpose(2, 1, 0, 3))


def lay_rhs(W):
    K, N = W.shape
    return np.ascontiguousarray(W.reshape(K // 128, 128, N).transpose(1, 0, 2))


def colvec(v):
    return np.ascontiguousarray(v.reshape(-1, 128).T)


class StopBuild(Exception):
    pass


class Builder:
    def chk(self, name):
        import os
        if os.environ.get("KSTOP", "") == name:
            raise StopBuild()

    def __init__(self, S, kinds):
        self.S = S
        self.kinds = kinds
        self.L = len(kinds)
        self.TC = S // 4
        self.NB = S // 512
        self.NST = self.TC // 512
        self.nc = bass.Bass("TRN2", target_bir_lowering=False)
        self.P = Prog(self.nc)
        self.ext_shapes = {}

    def ext(self, name, shape):
        self.ext_shapes[name] = tuple(shape)
        return self.nc.dram_tensor(name, list(shape), F32, kind="ExternalInput").ap()

    def scratch(self, name, shape, dt):
        return self.nc.dram_tensor(name, list(shape), dt, kind="Internal").ap()

    def build(self):
        P, nc, S, TC, L = self.P, self.nc, self.S, self.TC, self.L
        self.xT = self.ext("xT", [D, TC])
        self.outT = nc.dram_tensor("outT", [D, TC], F32, kind="ExternalOutput").ap()
        self.lng_d = self.ext("lng", [128, L * 2 * 16])
        self.lnb_d = self.ext("lnb", [128, L * 2 * 16])
        self.hres = self.scratch("hres", [D, TC], F32)
        self.hT_loc = self.scratch("hT_loc", [D, TC], BF16)
        self.HR = min(D, (1 << 20) // (TC * 2))
        self.NHQ = D // self.HR
        self.hT_all = [self.scratch(f"hT_all{q}", [4 * self.HR, TC], BF16) for q in range(self.NHQ)]
        self.mp = [self.scratch(f"mp{q}", [4 * 512, TC], F32) for q in range(4)]
        self.mixed = [self.scratch(f"mixed{q}", [512, TC], F32) for q in range(4)]
        self.ps_all = P.ps([128, 8, 512], F32)
        self.ones_bf = P.sb([128, 128], BF16)
        self.ones_f = P.sb([128, 128], F32)
        self.lng = P.sb([128, L * 2 * 16], F32)
        self.lnb = P.sb([128, L * 2 * 16], F32)
        P.op('dve', lambda e: e.memset(self.ones_bf[:], 1.0), writes=['ones_bf'])
        P.op('dve', lambda e: e.memset(self.ones_f[:], 1.0), writes=['ones_f'])
        self.eps_ln = P.sb([128, 1], F32)
        self.eps_rms = P.sb([128, 1], F32)
        P.op('dve', lambda e: e.memset(self.eps_ln[:], LN_EPS), writes=['eps'])
        P.op('dve', lambda e: e.memset(self.eps_rms[:], RMS_EPS), writes=['eps'])
        P.dma(self.lng[:], self.lng_d, writes=['lng'])
        P.dma(self.lnb[:], self.lnb_d, writes=['lnb'])
        for q in range(4):
            P.dma(self.hT_loc[q * 512:(q + 1) * 512, :], self.xT[q * 512:(q + 1) * 512, :], writes=['hT_loc'], eng='pool')
        self.gather_h()
        self.upb, self.dnb = [], []
        for l in range(L):
            up_e = self.ext(f"up{l}", [1024, 2048])
            dn_e = self.ext(f"dn{l}", [256, 8192])
            up_loc = self.scratch(f"upl{l}", [1024, 2048], BF16)
            dn_loc = self.scratch(f"dnl{l}", [256, 8192], BF16)
            upb = [self.scratch(f"upb{l}_{q}", [8 * 128, 2048], BF16) for q in range(8)]
            dnb = [self.scratch(f"dnb{l}_{q}", [8 * 32, 8192], BF16) for q in range(8)]
            for q in range(4):
                P.dma(up_loc[q * 256:(q + 1) * 256, :], up_e[q * 256:(q + 1) * 256, :], writes=[f'upl{l}'], eng='pool')
                P.dma(dn_loc[q * 64:(q + 1) * 64, :], dn_e[q * 64:(q + 1) * 64, :], writes=[f'dnl{l}'], eng='pool')
            self.upb.append(upb)
            self.dnb.append(dnb)
            self._mlpw_pending = getattr(self, "_mlpw_pending", []) + [(l, up_loc, dn_loc, upb, dnb)]
        try:
            self.chk('pro')
            self.layers()
        except StopBuild:
            while P.scopes:
                P.pop()
        P.emit()
        return nc

    def layers(self):
        P, L = self.P, self.L
        first = True
        for l, kind in enumerate(self.kinds):
            if kind == 0:
                self.mla(l)
            elif kind == 1:
                self.rwkv(l)
            else:
                self.swa(l)
            if first:
                for (ll, up_loc, dn_loc, upb, dnb) in self._mlpw_pending:
                    for q in range(8):
                        P.cc("AllGather", ALU.bypass, G8, up_loc[q * 128:(q + 1) * 128, :], upb[q], reads=[f'upl{ll}'], writes=[f'upb{ll}'])
                    for q in range(8):
                        P.cc("AllGather", ALU.bypass, G8, dn_loc[q * 32:(q + 1) * 32, :], dnb[q], reads=[f'dnl{ll}'], writes=[f'dnb{ll}'])
                first = False
            self.chk('m3')
            for q in range(4):
                P.cc("ReduceScatter", ALU.add, G4, self.mp[q], self.mixed[q],
                     reads=[('mp', tb) for tb in range(self.NB)], writes=['mixed'])
            self.chk('rs')
            self.p2(l, kind)
            self.chk('p2')
            if l < L - 1:
                self.gather_h()

    def gather_h(self):
        HR = self.HR
        for q in range(self.NHQ):
            self.P.cc("AllGather", ALU.bypass, G4, self.hT_loc[q * HR:(q + 1) * HR, :], self.hT_all[q],
                      reads=['hT_loc'] + [('hT_loc', st) for st in range(self.NST)], writes=['hT_all'])

    def load_hblock(self, tb, buf, key):
        r = (tb * 512) // self.TC
        off = (tb * 512) % self.TC
        HR = self.HR
        kq = HR // 128
        for q in range(self.NHQ):
            src = self.hT_all[q][r * HR:(r + 1) * HR, off:off + 512].rearrange("(k p) t -> p k t", p=128)
            self.P.dma(buf[:, q * kq:(q + 1) * kq, :], src, reads=['hT_all'], writes=[key])

    def outproj(self, tb, oT, okeys, wout, bank_mgr, mkpairs=None):
        P = self.P
        r = (tb * 512) // self.TC
        off = (tb * 512) % self.TC
        for half in range(4):
            ob = self.op_buf[(self.op_k) % 2]
            okey = ('opb', self.op_k % 2)
            self.op_k += 1
            for d8 in range(4):
                dc = half * 4 + d8
                ps, pk = bank_mgr.next()
                pairs = mkpairs(dc) if mkpairs else [(wout[:, dc, kc, :], oT[:, kc, :]) for kc in range(4)]
                P.mmacc(ps, pairs, reads=list(okeys) + ['wout'], writes=[pk])
                if d8 % 2 == 0:
                    P.act(ob[:, d8, :], ps, AF.Copy, reads=[pk], writes=[okey])
                else:
                    P.op('dve', lambda e, o=ob[:, d8, :], p=ps: e.tensor_copy(o, p), reads=[pk], writes=[okey])
            dst = self.mp[half][r * 512:(r + 1) * 512, off:off + 512].rearrange("(k p) t -> p k t", p=128)
            P.dma(dst, ob[:], reads=[okey], writes=[('mp', tb)])

    def mla(self, l):
        P, S, NB = self.P, self.S, self.NB
        NT = S // 128
        pre = f"L{l}_"
        win_d = self.ext(pre + "win", [9, 128, 16, 128])
        wqb_d = self.ext(pre + "wqb", [8, 128, 4, 128])
        wkn_d = self.ext(pre + "wkn", [4, 128, 4, 128])
        wv_d = self.ext(pre + "wv", [128, 4, 512])
        wout_d = self.ext(pre + "wout", [16, 128, 4, 128])
        gq_d = self.ext(pre + "gq", [128, 4])
        gkv_d = self.ext(pre + "gkv", [128, 4])
        if not hasattr(self, "cos_d"):
            self.cos_d = self.ext("cosT", [64, S])
            self.sin_d = self.ext("sinT", [64, S])
            self.tri_d = self.ext("tri", [128, 128])
        qN = self.scratch(pre + "qN", [4, 128, S], BF16)
        qR = self.scratch(pre + "qR", [4, 64, S], BF16)
        kN = self.scratch(pre + "kN", [4, 128, S], BF16)
        kR = self.scratch(pre + "kR", [64, S], BF16)
        Vd = self.scratch(pre + "V", [4, NT, 128, 128], BF16)
        oTd = self.scratch(pre + "oT", [512, S], BF16)
        scale = (128 + 64) ** -0.5

        P.push()
        win = P.sb([128, 9, 16, 128], BF16)
        wqb = P.sb([128, 8, 4, 128], BF16)
        wkn = P.sb([128, 4, 4, 128], BF16)
        wv = P.sb([128, 4, 512], BF16)
        gq = P.sb([128, 4], F32)
        gkv = P.sb([128, 4], F32)
        for m in range(9):
            P.dma(win[:, m], win_d[m], writes=['win'], eng='pool')
        P.dma(wqb[:], wqb_d.rearrange("m p k c -> p m k c"), writes=['wqb'], eng='pool')
        P.dma(wkn[:], wkn_d.rearrange("m p k c -> p m k c"), writes=['wkn'], eng='pool')
        P.dma(wv[:], wv_d, writes=['wv'], eng='pool')
        P.dma(gq[:], gq_d, writes=['gq'])
        P.dma(gkv[:], gkv_d, writes=['gkv'])
        hb = [P.sb([128, 16, 512], BF16) for _ in range(2)]
        lat = [P.sb([128, 4, 512], F32) for _ in range(2)]
        sq = [P.sb([128, 512], BF16) for _ in range(2)]
        nrm = [P.sb([128, 4, 512], BF16) for _ in range(2)]
        rstd = [P.sb([128, 512], F32) for _ in range(2)]
        cs = [P.sb([64, 2, 512], F32) for _ in range(2)]
        t1 = [P.sb([64, 512], F32) for _ in range(2)]
        t2 = [P.sb([64, 512], F32) for _ in range(2)]
        ob128 = [P.sb([128, 512], BF16) for _ in range(3)]
        ob64 = [P.sb([64, 512], BF16) for _ in range(3)]
        vb = [P.sb([128, 512], BF16) for _ in range(2)]
        bk = Banks(self.ps_all, range(8))
        k128 = 0
        k64 = 0
        kv_ = 0
        kt = 0

        def rope(psA, kA, psB, kB, csb, csk, dst, dkeys):
            nonlocal kt, k64
            a = t1[kt % 2]; b = t2[kt % 2]; ka = ('t1', kt % 2); kb_ = ('t2', kt % 2)
            kt += 1
            P.tt(a[:], psA[0:64, :], csb[:, 0, :], ALU.mult, reads=[kA, csk], writes=[ka])
            P.tt(b[:], psB[0:64, :], csb[:, 1, :], ALU.mult, reads=[kB, csk], writes=[kb_])
            o = ob64[k64 % 3]; ok = ('ob64', k64 % 3)
            k64 += 1
            P.tt(o[:], a[:], b[:], ALU.add, reads=[ka, kb_], writes=[ok])
            P.dma(dst, o[:], reads=[ok], writes=dkeys)

        for tb in range(NB):
            h = hb[tb % 2]; hk = ('hb', tb % 2)
            self.load_hblock(tb, h[:], hk)
            c = cs[tb % 2]; ck = ('cs', tb % 2)
            P.dma(c[:, 0, :], self.cos_d[:, tb * 512:(tb + 1) * 512], writes=[ck])
            P.dma(c[:, 1, :], self.sin_d[:, tb * 512:(tb + 1) * 512], writes=[ck])
            for which in range(2):
                lt = lat[which]; lk = ('lat', which)
                nr = nrm[which]; nk = ('nrm', which)
                g = gq if which == 0 else gkv
                for c4 in range(4):
                    mc = which * 4 + c4
                    ps, pk = bk.next()
                    P.mmacc(ps, [(win[:, mc, kc, :], h[:, kc, :]) for kc in range(16)], reads=['win', hk], writes=[pk])
                    P.op('dve', lambda e, o=lt[:, c4, :], p=ps: e.tensor_copy(o, p), reads=[pk], writes=[lk])
                rs = rstd[which]; rk = ('rstd', which)
                ps2, pk2 = bk.next()
                for c4 in range(4):
                    s_ = sq[c4 % 2]; sk = ('sq', c4 % 2)
                    P.act(s_[:], lt[:, c4, :], AF.Square, reads=[lk], writes=[sk])
                    P.mm(ps2, self.ones_bf[:], s_[:], start=(c4 == 0), stop=(c4 == 3), reads=['ones_bf', sk], writes=[pk2])
                P.act(rs[:], ps2, AF.Sqrt, reads=[pk2, 'eps'], writes=[rk], bias=self.eps_rms[:], scale=1.0 / 512)
                P.op('dve', lambda e, o=rs: e.reciprocal(o[:], o[:]), reads=[rk], writes=[rk])
                for c4 in range(4):
                    P.stt(nr[:, c4, :], lt[:, c4, :], g[:, c4:c4 + 1], rs[:], ALU.mult, ALU.mult,
                          reads=[lk, rk, 'gq', 'gkv'], writes=[nk])
            qn, qnk = nrm[0], ('nrm', 0)
            kvn, kvnk = nrm[1], ('nrm', 1)
            psA, kA = bk.next()
            P.mmacc(psA[0:64, :], [(win[:, 8, kc, 0:64], h[:, kc, :]) for kc in range(16)], reads=['win', hk], writes=[kA])
            psB, kB = bk.next()
            P.mmacc(psB[0:64, :], [(win[:, 8, kc, 64:128], h[:, kc, :]) for kc in range(16)], reads=['win', hk], writes=[kB])
            rope(psA, kA, psB, kB, c, ck, kR[:, tb * 512:(tb + 1) * 512], [('kR', tb)])
            for hh in range(4):
                ps, pk = bk.next()
                P.mmacc(ps, [(wqb[:, 2 * hh, c4, :], qn[:, c4, :]) for c4 in range(4)], reads=['wqb', qnk], writes=[pk])
                o = ob128[k128 % 3]; ok = ('ob128', k128 % 3); k128 += 1
                P.act(o[:], ps, AF.Copy, reads=[pk], writes=[ok])
                P.dma(qN[hh, :, tb * 512:(tb + 1) * 512], o[:], reads=[ok], writes=[('qN', hh, tb)])
                psA, kA = bk.next()
                P.mmacc(psA[0:64, :], [(wqb[:, 2 * hh + 1, c4, 0:64], qn[:, c4, :]) for c4 in range(4)], reads=['wqb', qnk], writes=[kA])
                psB, kB = bk.next()
                P.mmacc(psB[0:64, :], [(wqb[:, 2 * hh + 1, c4, 64:128], qn[:, c4, :]) for c4 in range(4)], reads=['wqb', qnk], writes=[kB])
                rope(psA, kA, psB, kB, c, ck, qR[hh, :, tb * 512:(tb + 1) * 512], [('qR', hh, tb)])
                ps, pk = bk.next()
                P.mmacc(ps, [(wkn[:, hh, c4, :], kvn[:, c4, :]) for c4 in range(4)], reads=['wkn', kvnk], writes=[pk])
                o = ob128[k128 % 3]; ok = ('ob128', k128 % 3); k128 += 1
                P.act(o[:], ps, AF.Copy, reads=[pk], writes=[ok])
                P.dma(kN[hh, :, tb * 512:(tb + 1) * 512], o[:], reads=[ok], writes=[('kN', hh, tb)])
            for tt_ in range(4):
                T = tb * 4 + tt_
                ps, pk = bk.next()
                P.mmacc(ps, [(kvn[:, c4, tt_ * 128:(tt_ + 1) * 128], wv[:, c4, :]) for c4 in range(4)], reads=['wv', kvnk], writes=[pk])
                o = vb[kv_ % 2]; ok = ('vb', kv_ % 2); kv_ += 1
                P.op('dve', lambda e, o=o, p=ps: e.tensor_copy(o[:], p), reads=[pk], writes=[ok])
                P.dma(Vd[:, T, :, :].rearrange("h p d -> p h d"), o[:].rearrange("p (h d) -> p h d", h=4),
                      reads=[ok], writes=[('V', T)])
        P.pop()
        self.chk('m1')

        P.push()
        tri = P.sb([128, 128], BF16)
        P.dma(tri[:], self.tri_d, writes=['tri'], eng='pool')
        krT = P.sb([64, S], BF16)
        P.dma(krT[:], kR, reads=[('kR', tb) for tb in range(NB)], writes=['krT'])
        knT = [P.sb([128, S], BF16) for _ in range(2)]
        Vh = [P.sb([128, NT, 128], BF16) for _ in range(2)]
        qnb = [P.sb([128, 512], BF16) for _ in range(2)]
        qrb = [P.sb([64, 512], BF16) for _ in range(2)]
        pT = [P.sb([128, 512], BF16) for _ in range(3)]
        rcp = [P.sb([128, 512], F32) for _ in range(2)]
        ob = [P.sb([128, 512], BF16) for _ in range(2)]
        bk = Banks(self.ps_all, range(4))
        acc_ids = [(4, 5), (6, 7)]
        kp = 0
        kq = 0
        for hh in range(4):
            kn = knT[hh % 2]; knk = ('knT', hh % 2)
            vh = Vh[hh % 2]; vk = ('Vh', hh % 2)
            P.dma(kn[:], kN[hh], reads=[('kN', hh, tb) for tb in range(NB)], writes=[knk])
            P.dma(vh[:], Vd[hh].rearrange("t p d -> p t d"), reads=[('V', T) for T in range(NT)], writes=[vk])
            for qb in range(NB):
                q0 = qb * 512
                qn_ = qnb[kq % 2]; qnk = ('qnb', kq % 2)
                qr_ = qrb[kq % 2]; qrk = ('qrb', kq % 2)
                a0, a1 = acc_ids[kq % 2]
                kq += 1
                P.dma(qn_[:], qN[hh, :, q0:q0 + 512], reads=[('qN', hh, qb)], writes=[qnk])
                P.dma(qr_[:], qR[hh, :, q0:q0 + 512], reads=[('qR', hh, qb)], writes=[qrk])
                o_ps = self.ps_all[:, a0, :]; ok_ = ('ps', a0)
                r_ps = self.ps_all[:, a1, :]; rk_ = ('ps', a1)
                nkt = (q0 + 512) // 128
                for kt_ in range(nkt):
                    k0 = kt_ * 128
                    o_ = max(0, (k0 - q0) // 128)
                    c0 = o_ * 128
                    ps, pk = bk.next()
                    P.mm(ps[:, c0:], kn[:, k0:k0 + 128], qn_[:, c0:], start=True, stop=False,
                         reads=[knk, qnk], writes=[pk], noinc=True)
                    P.mm(ps[:, c0:], krT[:, k0:k0 + 128], qr_[:, c0:], start=False, stop=True,
                         reads=['krT', qrk], writes=[pk])
                    p_ = pT[kp % 3]; ppk = ('pT', kp % 3); kp += 1
                    P.act(p_[:, c0:], ps[:, c0:], AF.Exp, reads=[pk], writes=[ppk], scale=scale)
                    if k0 >= q0:
                        P.tt(p_[:, c0:c0 + 128], p_[:, c0:c0 + 128], tri[:], ALU.mult, reads=[ppk, 'tri'], writes=[ppk])
                    last = (kt_ == nkt - 1)
                    P.mm(o_ps[:, c0:], vh[:, kt_, :], p_[:, c0:], start=(kt_ == 0), stop=last,
                         reads=[vk, ppk], writes=[ok_], noinc=True)
                    P.mm(r_ps[:, c0:], self.ones_bf[:], p_[:, c0:], start=(kt_ == 0), stop=last,
                         reads=['ones_bf', ppk], writes=[rk_])
                rc = rcp[qb % 2]; rck = ('rcp', qb % 2)
                P.op('dve', lambda e, o=rc, p=r_ps: e.reciprocal(o[:], p), reads=[rk_], writes=[rck])
                ot = ob[qb % 2]; otk = ('ob', qb % 2)
                P.tt(ot[:], o_ps, rc[:], ALU.mult, reads=[ok_, rck], writes=[otk])
                P.dma(oTd[hh * 128:(hh + 1) * 128, q0:q0 + 512], ot[:], reads=[otk], writes=[('oT', hh, qb)])
        P.pop()
        self.chk('m2')

        P.push()
        wout = P.sb([128, 16, 4, 128], BF16)
        P.dma(wout[:], wout_d.rearrange("m p k c -> p m k c"), writes=['wout'], eng='pool')
        oTb = [P.sb([128, 4, 512], BF16) for _ in range(2)]
        self.op_buf = [P.sb([128, 4, 512], F32) for _ in range(2)]
        self.op_k = 0
        bk = Banks(self.ps_all, range(8))
        for tb in range(NB):
            o = oTb[tb % 2]; ok = ('oTb', tb % 2)
            P.dma(o[:], oTd[:, tb * 512:(tb + 1) * 512].rearrange("(k p) t -> p k t", p=128),
                  reads=[('oT', hh, tb) for hh in range(4)], writes=[ok])
            self.outproj(tb, o, [ok], wout, bk)
        P.pop()

    def rwkv(self, l):
        P, S, NB = self.P, self.S, self.NB
        NCH = S // 128
        pre = f"L{l}_"
        CD = math.exp(-0.5)
        p4 = "m p k c -> p m k c"
        wr_d = self.ext(pre + "wr", [4, 128, 16, 128])
        wk_d = self.ext(pre + "wk", [4, 128, 16, 128])
        wv_d = self.ext(pre + "wv", [4, 128, 16, 128])
        w1_d = self.ext(pre + "w1", [1, 128, 16, 128])
        a1_d = self.ext(pre + "a1", [1, 128, 16, 128])
        g1_d = self.ext(pre + "g1", [2, 128, 16, 128])
        w2_d = self.ext(pre + "w2", [4, 128, 1, 128])
        a2_d = self.ext(pre + "a2", [4, 128, 1, 128])
        g2_d = self.ext(pre + "g2", [4, 128, 2, 128])
        wout_d = self.ext(pre + "wout", [16, 128, 4, 128])
        mix_d = self.ext(pre + "mix", [128, 96])
        cols_d = self.ext(pre + "cols", [128, 28])
        blk_d = self.ext("rw_blk", [128, 128])
        id_d = self.ext("rw_id", [128, 128])
        mm_d = self.ext("rw_mm", [128, 512])
        mt4_d = self.ext("rw_mt4", [128, 512])
        i4_d = self.ext("rw_i4", [128, 512])
        rst_d = self.ext("rw_rst", [128, 512])
        RAW = [self.scratch(pre + f"raw{q}", [512, S], F32) for q in range(5)]
        Gd = self.scratch(pre + "G", [512, S], BF16)
        CMd = [self.scratch(pre + f"cm{q}", [512, S], BF16) for q in range(4)]
        TMd = self.scratch(pre + "tm", [S, 3, 512], BF16)
        PCd = self.scratch(pre + "pc", [512, NCH], F32)
        Yd = self.scratch(pre + "Y", [512, S], F32)
        BVd = self.scratch(pre + "BV", [512, S], F32)
        fm = lambda ap: ap.rearrange("(k p) t -> p k t", p=128)

        P.push()

        def lw(d, shape, key):
            t = P.sb(shape, BF16)
            P.dma(t[:], d.rearrange(p4), writes=[key], eng='pool')
            return t
        wr = lw(wr_d, [128, 4, 16, 128], 'wr')
        wk = lw(wk_d, [128, 4, 16, 128], 'wk')
        wv = lw(wv_d, [128, 4, 16, 128], 'wv')
        w1 = lw(w1_d, [128, 1, 16, 128], 'w1')
        a1 = lw(a1_d, [128, 1, 16, 128], 'a1')
        g1 = lw(g1_d, [128, 2, 16, 128], 'g1')
        w2 = lw(w2_d, [128, 4, 1, 128], 'w2')
        a2 = lw(a2_d, [128, 4, 1, 128], 'a2')
        g2 = lw(g2_d, [128, 4, 2, 128], 'g2')
        mixc = P.sb([128, 96], F32)
        cols = P.sb([128, 28], F32)
        P.dma(mixc[:], mix_d, writes=['mixc'])
        P.dma(cols[:], cols_d, writes=['cols'])
        hb = [P.sb([128, 16, 512], BF16) for _ in range(2)]
        halo = P.sb([128, 16, 1], BF16)
        xx = P.sb([128, 16, 512], BF16)
        xi = [P.sb([128, 16, 512], BF16) for _ in range(2)]
        ev = [P.sb([128, 4, 512], F32) for _ in range(2)]
        evb = [P.sb([128, 4, 512], BF16) for _ in range(2)]
        lo = [P.sb([128, 2, 512], BF16) for _ in range(2)]
        bk = Banks(self.ps_all, range(8))
        P.op('dve', lambda e: e.memset(halo[:], 0.0), writes=['halo'])
        st8 = {'kx': 0, 'ke': 0, 'kl': 0, 'kb': 0}
        for tb in range(NB):
            csl = slice(tb * 512, (tb + 1) * 512)
            h = hb[tb % 2]; hk = ('hb', tb % 2)
            self.load_hblock(tb, h[:], hk)
            P.tt(xx[:, :, 1:512], h[:, :, 0:511], h[:, :, 1:512], ALU.subtract, reads=[hk], writes=['xx'])
            P.tt(xx[:, :, 0:1], halo[:], h[:, :, 0:1], ALU.subtract, reads=[hk, 'halo'], writes=['xx'])
            P.op('dve', lambda e, h=h: e.tensor_copy(halo[:], h[:, :, 511:512]), reads=[hk], writes=['halo'])

            def mkx(i, h=h, hk=hk):
                x = xi[st8['kx'] % 2]; xk = ('xi', st8['kx'] % 2); st8['kx'] += 1
                for kc in range(16):
                    P.stt(x[:, kc, :], xx[:, kc, :], mixc[:, i * 16 + kc:i * 16 + kc + 1], h[:, kc, :], ALU.mult, ALU.add,
                          reads=['xx', hk, 'mixc'], writes=[xk])
                return x, xk

            def big(i, w, wkey, q):
                x, xk = mkx(i)
                e = ev[st8['ke'] % 2]; ek = ('ev', st8['ke'] % 2); st8['ke'] += 1
                for mc in range(4):
                    ps, pk = bk.next()
                    P.mmacc(ps, [(w[:, mc, kc, :], x[:, kc, :]) for kc in range(16)], reads=[wkey, xk], writes=[pk])
                    if mc % 2 == 0:
                        P.act(e[:, mc, :], ps, AF.Copy, reads=[pk], writes=[ek])
                    else:
                        P.op('dve', lambda en, o=e[:, mc, :], p=ps: en.tensor_copy(o, p), reads=[pk], writes=[ek])
                P.dma(fm(RAW[q][:, csl]), e[:], reads=[ek], writes=[('RAW', q, tb)])

            def lora(i, wa, wakey, nh, wb, wbkey, f1, q, bcol):
                x, xk = mkx(i)
                lt = lo[st8['kl'] % 2]; lk = ('lo', st8['kl'] % 2); st8['kl'] += 1
                for m in range(nh):
                    ps, pk = bk.next()
                    P.mmacc(ps, [(wa[:, m, kc, :], x[:, kc, :]) for kc in range(16)], reads=[wakey, xk], writes=[pk])
                    P.act(lt[:, m, :], ps, f1, reads=[pk], writes=[lk])
                if q is not None:
                    e = ev[st8['ke'] % 2]; ek = ('ev', st8['ke'] % 2); st8['ke'] += 1
                else:
                    e = evb[st8['kb'] % 2]; ek = ('evb', st8['kb'] % 2); st8['kb'] += 1
                for mc in range(4):
                    ps, pk = bk.next()
                    P.mmacc(ps, [(wb[:, mc, m, :], lt[:, m, :]) for m in range(nh)], reads=[wbkey, lk], writes=[pk])
                    if q is not None:
                        P.act(e[:, mc, :], ps, AF.Sigmoid, reads=[pk, 'cols'], writes=[ek],
                              bias=cols[:, bcol * 4 + mc:bcol * 4 + mc + 1], scale=1.0)
                    else:
                        P.op('dve', lambda en, o=e[:, mc, :], p=ps: en.tensor_copy(o, p), reads=[pk], writes=[ek])
                if q is not None:
                    P.dma(fm(RAW[q][:, csl]), e[:], reads=[ek], writes=[('RAW', q, tb)])
                else:
                    P.dma(fm(Gd[:, csl]), e[:], reads=[ek], writes=[('G', tb)])

            big(0, wr, 'wr', 0)
            lora(1, w1, 'w1', 1, w2, 'w2', AF.Tanh, 3, 0)
            big(2, wk, 'wk', 1)
            big(3, wv, 'wv', 2)
            lora(4, a1, 'a1', 1, a2, 'a2', AF.Copy, 4, 1)
            lora(5, g1, 'g1', 2, g2, 'g2', AF.Sigmoid, None, None)
        P.pop()
        self.chk('r1a')

        P.push()
        cols = P.sb([128, 28], F32)
        P.dma(cols[:], cols_d, writes=['cols'])
        blk = P.sb([128, 128], BF16)
        ident = P.sb([128, 128], BF16)
        rst = P.sb([128, 512], F32)
        P.dma(blk[:], blk_d, writes=['blk'], eng='pool')
        P.dma(ident[:], id_d, writes=['ident'], eng='pool')
        P.dma(rst[:], rst_d, writes=['rst'])
        raw = [[P.sb([128, 4, 512], F32) for _ in range(2)] for _ in range(5)]
        T = {n: [P.sb([128, 512], F32) for _ in range(2)] for n in ('kk', 'rn', 'kp', 'bs', 'L', 'Ep', 'En', 'Epr')}
        sqb = [P.sb([128, 512], BF16) for _ in range(2)]
        rkb = [P.sb([128, 512], BF16) for _ in range(2)]
        outs = {n: [P.sb([128, 4, 512], BF16) for _ in range(2)] for n in ('At', 'Rt', 'Bt', 'Kt', 'Vb')}
        bvv = [P.sb([128, 4, 512], F32) for _ in range(2)]
        tmo = [P.sb([128, 3, 512], BF16) for _ in range(2)]
        pcs = P.sb([128, 4, NCH], F32)
        bk = Banks(self.ps_all, range(8))
        km = 0
        kt_ = 0
        for tb in range(NB):
            csl = slice(tb * 512, (tb + 1) * 512)
            par = tb % 2
            for q in range(5):
                P.dma(raw[q][par][:], fm(RAW[q][:, csl]), reads=[('RAW', q, tb)], writes=[('raw', q, par)])
            rk_ = [('raw', q, par) for q in range(5)]
            O = {n: outs[n][par] for n in outs}
            OK = {n: ('out', n, par) for n in outs}
            bv_ = bvv[par]; bvk = ('bvv', par)
            for mc in range(4):
                m2 = km % 2; km += 1
                r_ = raw[0][par][:, mc, :]; k_ = raw[1][par][:, mc, :]; v_ = raw[2][par][:, mc, :]
                sg_ = raw[3][par][:, mc, :]; a_ = raw[4][par][:, mc, :]
                t = {n: T[n][m2] for n in T}
                tk = {n: ('T', n, m2) for n in T}
                c_ = lambda v: cols[:, v * 4 + mc:v * 4 + mc + 1]
                P.ts(t['kk'][:], k_, c_(2), None, ALU.mult, reads=[rk_[1], 'cols'], writes=[tk['kk']])
                sq = sqb[m2]; sqk = ('sqb', m2)
                P.act(sq[:], t['kk'][:], AF.Square, reads=[tk['kk']], writes=[sqk])
                ps, pk = bk.next()
                P.mm(ps, blk[:], sq[:], reads=['blk', sqk], writes=[pk])
                P.act(t['rn'][:], ps, AF.Sqrt, reads=[pk], writes=[tk['rn']])
                P.ts(t['rn'][:], t['rn'][:], 1e-12, None, ALU.max, reads=[tk['rn']], writes=[tk['rn']])
                P.op('dve', lambda e, o=t['rn']: e.reciprocal(o[:], o[:]), reads=[tk['rn']], writes=[tk['rn']])
                P.tt(t['kk'][:], t['kk'][:], t['rn'][:], ALU.mult, reads=[tk['kk'], tk['rn']], writes=[tk['kk']])
                P.ts(t['kp'][:], a_, 1.0, c_(3), ALU.subtract, ALU.mult, reads=[rk_[4], 'cols'], writes=[tk['kp']])
                P.stt(t['kp'][:], t['kp'][:], 1.0, k_, ALU.add, ALU.mult, reads=[tk['kp'], rk_[1]], writes=[tk['kp']])
                P.tt(t['bs'][:], t['kk'][:], a_, ALU.mult, reads=[tk['kk'], rk_[4]], writes=[tk['bs']])
                P.op('dve', lambda e, o=t['L'], s=sg_: e.tensor_tensor_scan(o[:], rst[:], s, 0.0, ALU.mult, ALU.add),
                     reads=[rk_[3], 'rst'], writes=[tk['L']])
                P.act(t['Ep'][:], t['L'][:], AF.Exp, reads=[tk['L']], writes=[tk['Ep']], scale=-CD)
                P.act(t['En'][:], t['L'][:], AF.Exp, reads=[tk['L']], writes=[tk['En']], scale=CD)
                P.tt(t['Epr'][:], t['L'][:], sg_, ALU.subtract, reads=[tk['L'], rk_[3]], writes=[tk['Epr']])
                P.act(t['Epr'][:], t['Epr'][:], AF.Exp, reads=[tk['Epr']], writes=[tk['Epr']], scale=-CD)
                P.stt(O['At'][:, mc, :], t['kk'][:], -1.0, t['Epr'][:], ALU.mult, ALU.mult, reads=[tk['kk'], tk['Epr']], writes=[OK['At']])
                P.tt(O['Rt'][:, mc, :], r_, t['Ep'][:], ALU.mult, reads=[rk_[0], tk['Ep']], writes=[OK['Rt']])
                P.tt(O['Bt'][:, mc, :], t['bs'][:], t['En'][:], ALU.mult, reads=[tk['bs'], tk['En']], writes=[OK['Bt']])
                P.tt(O['Kt'][:, mc, :], t['kp'][:], t['En'][:], ALU.mult, reads=[tk['kp'], tk['En']], writes=[OK['Kt']])
                P.act(O['Vb'][:, mc, :], v_, AF.Copy, reads=[rk_[2]], writes=[OK['Vb']])
                P.act(pcs[:, mc, tb * 4:(tb + 1) * 4], t['Ep'][:].rearrange("p (c n) -> p c n", n=128)[:, :, 127], AF.Copy,
                      reads=[tk['Ep']], writes=['pcs'])
                rk2 = rkb[m2]; rkk = ('rkb', m2)
                P.stt(rk2[:], r_, c_(4), t['kp'][:], ALU.mult, ALU.mult, reads=[rk_[0], tk['kp'], 'cols'], writes=[rkk])
                ps, pk = bk.next()
                P.mm(ps, blk[:], rk2[:], reads=['blk', rkk], writes=[pk])
                P.tt(bv_[:, mc, :], ps, v_, ALU.mult, reads=[pk, rk_[2]], writes=[bvk])
            P.dma(fm(BVd[:, csl]), bv_[:], reads=[bvk], writes=[('BV', tb)])
            for q, n in enumerate(('At', 'Rt', 'Bt', 'Kt')):
                P.dma(fm(CMd[q][:, csl]), O[n][:], reads=[OK[n]], writes=[('CM', q, tb)])
            for tt_ in range(4):
                to = tmo[kt_ % 2]; tok = ('tmo', kt_ % 2); kt_ += 1
                for q, n in enumerate(('Bt', 'Kt', 'Vb')):
                    ps, pk = bk.next()
                    for mc in range(4):
                        P.mm(ps[:, mc * 128:(mc + 1) * 128], O[n][:, mc, tt_ * 128:(tt_ + 1) * 128], ident[:],
                             reads=[OK[n], 'ident'], writes=[pk], noinc=(mc < 3))
                    if q == 1:
                        P.op('dve', lambda e, o=to[:, q, :], p=ps: e.tensor_copy(o, p), reads=[pk], writes=[tok])
                    else:
                        P.act(to[:, q, :], ps, AF.Copy, reads=[pk], writes=[tok])
                r0 = tb * 512 + tt_ * 128
                P.dma(TMd[r0:r0 + 128], to[:], reads=[tok], writes=[('TM', tb)])
        P.dma(PCd.rearrange("(k p) c -> p k c", p=128), pcs[:], reads=['pcs'], writes=['PCd'])
        P.pop()
        self.chk('r1b')

        P.push()
        MM = P.sb([128, 4, 128], F32)
        MT4 = P.sb([128, 4, 128], F32)
        I4 = P.sb([128, 4, 128], F32)
        P.dma(MM[:].rearrange("p a n -> p (a n)"), mm_d, writes=['MM'])
        P.dma(MT4[:].rearrange("p a n -> p (a n)"), mt4_d, writes=['MT4'])
        P.dma(I4[:].rearrange("p a n -> p (a n)"), i4_d, writes=['I4'])
        pc = P.sb([64, 8, NCH], F32)
        P.dma(pc[:], PCd.rearrange("(h p) c -> p h c", p=64), reads=['PCd'], writes=['pc'])
        cm = [P.sb([64, 8, 4, 512], BF16) for _ in range(2)]
        tm = [P.sb([128, 4, 3, 512], BF16) for _ in range(2)]
        AM = [P.sb([128, 8, 4, 128], BF16) for _ in range(2)]
        Ti = [P.sb([128, 8, 128], BF16) for _ in range(2)]
        XX = [[P.sb([128, 4, 2, 128], BF16) for _ in range(2)] for _ in range(2)]
        QT = [[P.sb([128, 4, 128], BF16) for _ in range(2)] for _ in range(2)]
        Xs = P.sb([128, 512], BF16)
        U = P.sb([128, 512], BF16)
        ST = P.sb([64, 8, 64], F32)
        STp = P.sb([64, 8, 64], F32)
        STb = P.sb([64, 8, 64], BF16)
        yb = [P.sb([64, 8, 512], F32) for _ in range(2)]
        P.op('dve', lambda e: e.memset(ST[:], 0.0), writes=['ST'])
        P.op('dve', lambda e: e.memset(STp[:], 0.0), writes=['STp'])
        P.op('dve', lambda e: e.memset(STb[:], 0.0), writes=['STb'])
        pv = self.ps_all[:, 0:2, :].rearrange("p b (g a n) -> p (b g) a n", g=2, a=2)
        b2 = self.ps_all[:, 2, :].rearrange("p (h n) -> p h n", h=4)
        b3 = self.ps_all[:, 3, :].rearrange("p (a n) -> p a n", a=4)
        PV = [('ps', 0), ('ps', 1)]

        def load_block(b):
            par = b % 2
            csl = slice(b * 512, (b + 1) * 512)
            for q in range(4):
                P.dma(cm[par][:, :, q, :], CMd[q][:, csl].rearrange("(h p) t -> p h t", p=64),
                      reads=[('CM', q, b)], writes=[('cm', par)])
            P.dma(tm[par][:], TMd[b * 512:(b + 1) * 512].rearrange("(c p) q f -> p c q f", p=128),
                  reads=[('TM', b)], writes=[('tm', par)])

        def pre_(c):
            cpar = c % 2
            b = c // 4
            cl = c % 4
            cmt = cm[b % 2]; cmk = ('cm', b % 2)
            cs = slice(cl * 128, (cl + 1) * 128)
            AMt = AM[cpar]
            for g in range(2):
                amk = ('AM', cpar, g)
                hs = slice(g * 4, g * 4 + 4)
                for hl in range(4):
                    hh = g * 4 + hl
                    P.mm(b3[:, 0:2, :], cmt[:, hh, 2, cs], cmt[:, hh, 0:2, cs], reads=[cmk], writes=[('ps', 3)], noinc=True)
                    P.mm(b3[:, 2:4, :], cmt[:, hh, 3, cs], cmt[:, hh, 0:2, cs], reads=[cmk], writes=[('ps', 3)])
                    P.tt(AMt[:, hh, :, :], b3, MM[:], ALU.mult, reads=[('ps', 3), 'MM'], writes=[amk])
                    P.mm(b2[:, hl, :], cmt[:, hh, 0, cs], cmt[:, hh, 2, cs], reads=[cmk], writes=[('ps', 2)], noinc=(hl < 3))
                q0 = QT[g][0]; q0k = ('QT', g, 0)
                P.tt(q0[:], b2, MT4[:], ALU.mult, reads=[('ps', 2), 'MT4'], writes=[q0k])
                yield
                x1 = XX[g][1]; x1k = ('XX', g, 1)
                q1 = QT[g][1]; q1k = ('QT', g, 1)
                for hl in range(4):
                    hh = g * 4 + hl
                    P.mm(pv[:, hl, 0, :], q0[:, hl, :], AMt[:, hh, 0, :], reads=[q0k, amk], writes=PV, noinc=True)
                    P.mm(b2[:, hl, :], AMt[:, hh, 0, :], q0[:, hl, :], reads=[q0k, amk], writes=[('ps', 2)], noinc=(hl < 3))
                P.act(x1[:, :, 0, :], pv[:, :, 0, :], AF.Copy, reads=PV, writes=[x1k])
                P.tt(x1[:, :, 1, :], AMt[:, hs, 0, :], I4[:], ALU.add, reads=[amk, 'I4'], writes=[x1k])
                P.act(q1[:], b2, AF.Copy, reads=[('ps', 2)], writes=[q1k])
                yield
                for i in range(1, 6):
                    xi_ = XX[g][i % 2]; xik = ('XX', g, i % 2)
                    xn = XX[g][(i + 1) % 2]; xnk = ('XX', g, (i + 1) % 2)
                    qi = QT[g][i % 2]; qik = ('QT', g, i % 2)
                    qn = QT[g][(i + 1) % 2]; qnk = ('QT', g, (i + 1) % 2)
                    for hl in range(4):
                        P.mm(pv[:, hl, :, :], qi[:, hl, :], xi_[:, hl, :, :], reads=[qik, xik], writes=PV, noinc=True)
                        P.mm(b2[:, hl, :], xi_[:, hl, 0, :], qi[:, hl, :], reads=[qik, xik], writes=[('ps', 2)], noinc=(hl < 3))
                    P.act(xn[:, :, 0, :], pv[:, :, 0, :], AF.Copy, reads=PV, writes=[xnk])
                    P.tt(xn[:, :, 1, :], pv[:, :, 1, :], xi_[:, :, 1, :], ALU.add, reads=PV + [xik], writes=[xnk])
                    P.act(qn[:], b2, AF.Copy, reads=[('ps', 2)], writes=[qnk])
                    yield
                x6 = XX[g][0]; x6k = ('XX', g, 0)
                q6 = QT[g][0]; q6k = ('QT', g, 0)
                for hl in range(4):
                    P.mm(pv[:, hl, 0, :], q6[:, hl, :], x6[:, hl, 1, :], reads=[q6k, x6k], writes=PV, noinc=(hl < 3))
                P.tt(Ti[cpar][:, hs, :], pv[:, :, 0, :], x6[:, :, 1, :], ALU.add, reads=PV + [x6k], writes=[('Ti', cpar, g)])
                yield

        def seq_(c):
            cpar = c % 2
            b = c // 4
            cl = c % 4
            cmt = cm[b % 2]; cmk = ('cm', b % 2)
            tmt = tm[b % 2]; tmk = ('tm', b % 2)
            cs = slice(cl * 128, (cl + 1) * 128)
            AMt = AM[cpar]
            amks = [('AM', cpar, 0), ('AM', cpar, 1)]
            tiks = [('Ti', cpar, 0), ('Ti', cpar, 1)]
            b4 = self.ps_all[:, 4, :]
            b5 = self.ps_all[:, 5, :]
            hsl = lambda hh: slice(hh * 64, (hh + 1) * 64)
            for hh in range(8):
                P.mm(b4[:, hsl(hh)], cmt[:, hh, 0, cs], STb[:, hh, :], start=True, stop=False,
                     reads=[cmk, 'STb'], writes=[('ps', 4)], noinc=True)
                P.mm(b4[:, hsl(hh)], AMt[:, hh, 2, :], tmt[:, cl, 2, hsl(hh)], start=False, stop=True,
                     reads=amks + [tmk], writes=[('ps', 4)], noinc=(hh < 7))
            P.act(Xs[:], b4, AF.Copy, reads=[('ps', 4)], writes=['Xs'])
            yield
            for hh in range(8):
                P.mm(b5[:, hsl(hh)], Ti[cpar][:, hh, :], Xs[:, hsl(hh)], reads=tiks + ['Xs'], writes=[('ps', 5)], noinc=(hh < 7))
            P.op('dve', lambda e: e.tensor_copy(U[:], b5), reads=[('ps', 5)], writes=['U'])
            yield
            for hh in range(8):
                yo = self.ps_all[0:64, 6 + hh // 4, (hh % 4) * 128:(hh % 4 + 1) * 128]
                yk = [('ps', 6 + hh // 4)]
                P.mm(yo, STb[:, hh, :], cmt[:, hh, 1, cs], start=True, stop=False, reads=['STb', cmk], writes=yk, noinc=True)
                P.mm(yo, U[:, hsl(hh)], AMt[:, hh, 1, :], start=False, stop=False, reads=['U'] + amks, writes=yk, noinc=True)
                P.mm(yo, tmt[:, cl, 2, hsl(hh)], AMt[:, hh, 3, :], start=False, stop=True, reads=[tmk] + amks, writes=yk, noinc=True)
                zo = self.ps_all[0:64, 4, hsl(hh)]
                P.mm(zo, tmt[:, cl, 0, hsl(hh)], U[:, hsl(hh)], start=True, stop=False, reads=[tmk, 'U'], writes=[('ps', 4)], noinc=True)
                P.mm(zo, tmt[:, cl, 1, hsl(hh)], tmt[:, cl, 2, hsl(hh)], start=False, stop=True, reads=[tmk], writes=[('ps', 4)],
                     noinc=(hh < 7))
            ybt = yb[b % 2]; ybk = ('yb', b % 2)
            P.act(ybt[:, 0:4, cs], self.ps_all[0:64, 6, :].rearrange("p (h n) -> p h n", h=4), AF.Copy,
                  reads=[('ps', 6)], writes=[ybk])
            P.op('dve', lambda e, o=ybt[:, 4:8, cs]: e.tensor_copy(o, self.ps_all[0:64, 7, :].rearrange("p (h n) -> p h n", h=4)),
                 reads=[('ps', 7)], writes=[ybk])
            for hh in range(8):
                P.stt(ST[:, hh, :], self.ps_all[0:64, 4, hsl(hh)], pc[:, hh, c:c + 1], STp[:, hh, :], ALU.mult, ALU.add,
                      reads=[('ps', 4), 'pc', 'STp'], writes=['ST'])
            P.act(STb[:].rearrange("p h v -> p (h v)"), ST[:].rearrange("p h v -> p (h v)"), AF.Copy, reads=['ST'], writes=['STb'])
            if c + 1 < NCH:
                for hh in range(8):
                    P.ts(STp[:, hh, :], ST[:, hh, :], pc[:, hh, c + 1:c + 2], None, ALU.mult, reads=['ST', 'pc'], writes=['STp'])
            if cl == 3:
                P.dma(Yd[:, b * 512:(b + 1) * 512].rearrange("(h p) t -> p h t", p=64), ybt[:], reads=[ybk], writes=[('Y', b)])
            yield

        load_block(0)
        for _ in pre_(0):
            pass
        for c in range(NCH):
            if c + 1 < NCH and (c + 1) % 4 == 0:
                load_block((c + 1) // 4)
            sq_ = seq_(c)
            pr_ = pre_(c + 1) if c + 1 < NCH else iter(())
            for n in (5, 6, 5):
                next(sq_, None)
                for _ in range(n):
                    next(pr_, None)
            for _ in sq_:
                pass
            for _ in pr_:
                pass
        P.pop()
        self.chk('r2')

        P.push()
        cols = P.sb([128, 28], F32)
        P.dma(cols[:], cols_d, writes=['cols'])
        blkf = P.sb([128, 128], F32)
        P.dma(blkf[:], blk_d, writes=['blkf'])
        epsg = P.sb([128, 1], F32)
        P.op('dve', lambda e: e.memset(epsg[:], 64e-5), writes=['epsg'])
        wout = P.sb([128, 16, 4, 128], BF16)
        P.dma(wout[:], wout_d.rearrange(p4), writes=['wout'], eng='pool')
        yt = [P.sb([128, 4, 512], F32) for _ in range(2)]
        bvt = [P.sb([128, 4, 512], F32) for _ in range(2)]
        gt = [P.sb([128, 4, 512], BF16) for _ in range(2)]
        oT = [P.sb([128, 4, 512], BF16) for _ in range(2)]
        sqf = [P.sb([128, 512], F32) for _ in range(2)]
        mt = [P.sb([128, 512], F32) for _ in range(2)]
        vt = [P.sb([128, 512], F32) for _ in range(2)]
        self.op_buf = [P.sb([128, 4, 512], F32) for _ in range(2)]
        self.op_k = 0
        bk = Banks(self.ps_all, range(8))
        km = 0
        for tb in range(NB):
            csl = slice(tb * 512, (tb + 1) * 512)
            par = tb % 2
            y_ = yt[par]; yk = ('yt', par)
            P.dma(y_[:], fm(Yd[:, csl]), reads=[('Y', tb)], writes=[yk])
            P.dma(bvt[par][:], fm(BVd[:, csl]), reads=[('BV', tb)], writes=[('bvt', par)])
            P.dma(gt[par][:], fm(Gd[:, csl]), reads=[('G', tb)], writes=[('gt', par)])
            o_ = oT[par]; ok = ('oTr', par)
            for mc in range(4):
                m2 = km % 2; km += 1
                c_ = lambda v: cols[:, v * 4 + mc:v * 4 + mc + 1]
                sq = sqf[m2]; sqk = ('sqf', m2)
                m_ = mt[m2]; mk_ = ('mt', m2)
                v_ = vt[m2]; vk_ = ('vt', m2)
                P.act(sq[:], y_[:, mc, :], AF.Square, reads=[yk], writes=[sqk])
                p1, k1 = bk.next()
                P.mm(p1, blkf[:], y_[:, mc, :], reads=['blkf', yk], writes=[k1])
                p2, k2 = bk.next()
                P.mm(p2, blkf[:], sq[:], reads=['blkf', sqk], writes=[k2])
                P.act(m_[:], p1, AF.Copy, reads=[k1], writes=[mk_], scale=1.0 / 64)
                P.act(v_[:], p1, AF.Square, reads=[k1], writes=[vk_], scale=1.0 / 64)
                P.stt(v_[:], p2, 1.0 / 64, v_[:], ALU.mult, ALU.subtract, reads=[k2, vk_], writes=[vk_])
                P.act(v_[:], v_[:], AF.Sqrt, reads=[vk_, 'epsg'], writes=[vk_], bias=epsg[:], scale=1.0)
                P.op('dve', lambda e, o=v_: e.reciprocal(o[:], o[:]), reads=[vk_], writes=[vk_])
                P.tt(sq[:], y_[:, mc, :], m_[:], ALU.subtract, reads=[yk, mk_], writes=[sqk])
                P.tt(sq[:], sq[:], v_[:], ALU.mult, reads=[sqk, vk_], writes=[sqk])
                P.ts(sq[:], sq[:], c_(5), c_(6), ALU.mult, ALU.add, reads=[sqk, 'cols'], writes=[sqk])
                P.tt(sq[:], sq[:], bvt[par][:, mc, :], ALU.add, reads=[sqk, ('bvt', par)], writes=[sqk])
                P.tt(o_[:, mc, :], sq[:], gt[par][:, mc, :], ALU.mult, reads=[sqk, ('gt', par)], writes=[ok])
            self.outproj(tb, o_, [ok], wout, bk)
        P.pop()

    def swa(self, l):
        P, S, NB = self.P, self.S, self.NB
        NT = S // 128
        pre = f"L{l}_"
        wq_d = self.ext(pre + "wq", [4, 128, 16, 128])
        wkv_d = self.ext(pre + "wkv", [1, 128, 16, 128])
        wv_d = self.ext(pre + "wvr", [128, 16, 64])
        bq_d = self.ext(pre + "bq", [64, 8])
        bk_d = self.ext(pre + "bk", [64, 1])
        bv_d = self.ext(pre + "bv", [128, 64])
        wout_d = self.ext(pre + "wout", [16, 128, 4, 128])
        self.bout_d = self.ext(pre + "bout", [128, 16])
        bias_d = self.ext(pre + "biasg", [128, 2 * 8 * 128])
        mask_d = self.ext(pre + "wmask", [128, 2 * 8 * 128])
        sink_d = self.ext(pre + "sinks", [64, 8])
        qS = self.scratch(pre + "qS", [8, 64, S], BF16)
        kS = self.scratch(pre + "kS", [64, S], BF16)
        Vs = self.scratch(pre + "Vs", [NT, 128, 64], BF16)
        oS = self.scratch(pre + "oS", [8, 64, S], BF16)
        scale = 64 ** -0.5

        P.push()
        wq = P.sb([128, 4, 16, 128], BF16)
        wkv = P.sb([128, 16, 128], BF16)
        wvr = P.sb([128, 16, 64], BF16)
        bq = P.sb([64, 8], F32); bkk = P.sb([64, 1], F32); bv = P.sb([128, 64], F32)
        P.dma(wq[:], wq_d.rearrange("m p k c -> p m k c"), writes=['wq'], eng='pool')
        P.dma(wkv[:], wkv_d[0], writes=['wkv'], eng='pool')
        P.dma(wvr[:], wv_d, writes=['wvr'], eng='pool')
        P.dma(bq[:], bq_d, writes=['bq']); P.dma(bkk[:], bk_d, writes=['bq']); P.dma(bv[:], bv_d, writes=['bq'])
        hb = [P.sb([128, 16, 512], BF16) for _ in range(2)]
        ob = [P.sb([64, 512], BF16) for _ in range(3)]
        vb = [P.sb([128, 64], BF16) for _ in range(2)]
        bk = Banks(self.ps_all, range(8))
        ko = 0; kv_ = 0
        for tb in range(NB):
            h = hb[tb % 2]; hk = ('hb', tb % 2)
            self.load_hblock(tb, h[:], hk)
            for hh in range(9):
                ps, pk = bk.next()
                if hh < 8:
                    pairs = [(wq[:, hh // 2, kc, (hh % 2) * 64:(hh % 2) * 64 + 64], h[:, kc, :]) for kc in range(16)]
                    bcol = bq[:, hh:hh + 1]
                    dst = qS[hh, :, tb * 512:(tb + 1) * 512]; dk = ('qS', tb)
                else:
                    pairs = [(wkv[:, kc, 0:64], h[:, kc, :]) for kc in range(16)]
                    bcol = bkk[:, 0:1]
                    dst = kS[:, tb * 512:(tb + 1) * 512]; dk = ('kS', tb)
                P.mmacc(ps[0:64, :], pairs, reads=['wq', 'wkv', hk], writes=[pk])
                o = ob[ko % 3]; ok = ('ob', ko % 3); ko += 1
                P.act(o[:], ps[0:64, :], AF.Identity, reads=[pk, 'bq'], writes=[ok], bias=bcol, scale=1.0)
                P.dma(dst, o[:], reads=[ok], writes=[dk])
            for tt_ in range(4):
                T = tb * 4 + tt_
                ps, pk = bk.next()
                P.mmacc(ps[:, 0:64], [(h[:, kc, tt_ * 128:(tt_ + 1) * 128], wvr[:, kc, :]) for kc in range(16)],
                        reads=['wvr', hk], writes=[pk])
                o = vb[kv_ % 2]; ok = ('vb', kv_ % 2); kv_ += 1
                P.tt(o[:], ps[:, 0:64], bv[:], ALU.add, reads=[pk, 'bq'], writes=[ok])
                P.dma(Vs[T], o[:], reads=[ok], writes=[('Vs', T)])
        P.pop()
        self.chk('s1')

        P.push()
        E = P.sb([128, 2, 8, 128], BF16)
        Ef = P.sb([128, 2 * 8 * 128], F32)
        Mf = P.sb([128, 2 * 8 * 128], F32)
        P.dma(Ef[:], bias_d, writes=['Ef']); P.dma(Mf[:], mask_d, writes=['Mf'])
        P.act(Ef[:], Ef[:], AF.Exp, reads=['Ef'], writes=['Ef'])
        P.tt(E[:].rearrange("p a h q -> p (a h q)"), Ef[:], Mf[:], ALU.mult, reads=['Ef', 'Mf'], writes=['E'])
        es = P.sb([64, 8], F32); esb = P.sb([64, 8, 128], F32); zer = P.sb([64, 128], F32)
        P.dma(es[:], sink_d, writes=['es'])
        P.act(es[:], es[:], AF.Exp, reads=['es'], writes=['es'])
        P.op('dve', lambda e: e.memset(zer[:], 0.0), writes=['zer'])
        for hh in range(8):
            P.ts(esb[:, hh, :], zer[:], es[:, hh:hh + 1], None, ALU.add, reads=['zer', 'es'], writes=['esb'])
        kT = P.sb([64, S], BF16)
        Vt = P.sb([128, NT, 64], BF16)
        P.dma(kT[:], kS, reads=[('kS', tb) for tb in range(NB)], writes=['kT'])
        P.dma(Vt[:], Vs.rearrange("t p d -> p t d"), reads=[('Vs', T) for T in range(NT)], writes=['Vt'])
        qa = [P.sb([64, 8, 512], BF16) for _ in range(2)]
        oa = [P.sb([64, 8, 512], BF16) for _ in range(2)]
        pT = [P.sb([128, 8, 128], BF16) for _ in range(2)]
        rc = [P.sb([64, 8, 128], F32) for _ in range(2)]
        kp = 0
        for qb in range(NB):
            q_ = qa[qb % 2]; qk = ('qa', qb % 2)
            o_ = oa[qb % 2]; ok = ('oa', qb % 2)
            P.dma(q_[:], qS[:, :, qb * 512:(qb + 1) * 512].rearrange("h p t -> p h t"), reads=[('qS', qb)], writes=[qk])
            for sb_ in range(4):
                blk = qb * 4 + sb_
                tiles = ([blk - 1] if blk > 0 else []) + [blk]
                for ti, kt_ in enumerate(tiles):
                    typ = 1 if kt_ == blk else 0
                    b0 = (kp % 2) * 2
                    sk = [('ps', b0), ('ps', b0 + 1)]
                    for hh in range(8):
                        P.mm(self.ps_all[:, b0 + hh // 4, (hh % 4) * 128:(hh % 4 + 1) * 128], kT[:, kt_ * 128:(kt_ + 1) * 128],
                             q_[:, hh, sb_ * 128:(sb_ + 1) * 128], reads=['kT', qk], writes=sk, noinc=(hh < 7))
                    p_ = pT[kp % 2]; ppk = ('pT', kp % 2); kp += 1
                    P.act(p_[:].rearrange("p h q -> p (h q)"), self.ps_all[:, b0:b0 + 2, :].rearrange("p a n -> p (a n)"),
                          AF.Exp, reads=sk, writes=[ppk], scale=scale)
                    P.tt(p_[:], p_[:], E[:, typ, :, :], ALU.mult, reads=[ppk, 'E'], writes=[ppk])
                    first = (ti == 0); lastt = (ti == len(tiles) - 1)
                    for half in range(2):
                        rhs = p_[:, half * 4:(half + 1) * 4, :].rearrange("p h q -> p (h q)")
                        P.mm(self.ps_all[0:64, 4 + half, :], Vt[:, kt_, :], rhs, start=first, stop=lastt,
                             reads=['Vt', ppk], writes=[('ps', 4 + half)], noinc=True)
                        P.mm(self.ps_all[0:64, 6 + half, :], self.ones_bf[:, 0:64], rhs, start=first, stop=lastt,
                             reads=['ones_bf', ppk], writes=[('ps', 6 + half)])
                r_ = rc[blk % 2]; rk = ('rc', blk % 2)
                P.tt(r_[:].rearrange("p h q -> p (h q)"), self.ps_all[0:64, 6:8, :].rearrange("p a n -> p (a n)"),
                     esb[:].rearrange("p h q -> p (h q)"), ALU.add, reads=[('ps', 6), ('ps', 7), 'esb'], writes=[rk])
                P.op('dve', lambda e, r=r_: e.reciprocal(r[:], r[:]), reads=[rk], writes=[rk])
                for half in range(2):
                    P.tt(o_[:, half * 4:(half + 1) * 4, sb_ * 128:(sb_ + 1) * 128],
                         self.ps_all[0:64, 4 + half, :].rearrange("p (h q) -> p h q", h=4),
                         r_[:, half * 4:(half + 1) * 4, :], ALU.mult, reads=[('ps', 4 + half), rk], writes=[ok])
            P.dma(oS[:, :, qb * 512:(qb + 1) * 512].rearrange("h p t -> p h t"), o_[:], reads=[ok], writes=[('oS', qb)])
        P.pop()
        self.chk('s2')

        P.push()
        wout = P.sb([128, 16, 4, 128], BF16)
        P.dma(wout[:], wout_d.rearrange("m p k c -> p m k c"), writes=['wout'], eng='pool')
        oTb = [P.sb([128, 4, 512], BF16) for _ in range(2)]
        self.op_buf = [P.sb([128, 4, 512], F32) for _ in range(2)]
        self.op_k = 0
        bk = Banks(self.ps_all, range(8))
        oSf = oS.rearrange("h p t -> (h p) t")
        for tb in range(NB):
            o = oTb[tb % 2]; ok = ('oTb', tb % 2)
            P.dma(o[:], oSf[:, tb * 512:(tb + 1) * 512].rearrange("(k p) t -> p k t", p=128), reads=[('oS', tb)], writes=[ok])
            self.outproj(tb, o, [ok], wout, bk)
        P.pop()
        self.chk('s3')

    def layernorm(self, R, Rk, Rb, Rbk, gcol, bcol, bk, tmp):
        P = self.P
        s1, k1 = self.ps_all[:, 6, :], ('ps', 6)
        s2, k2 = self.ps_all[:, 7, :], ('ps', 7)
        sqs, m, var = tmp
        for dc in range(16):
            s_ = sqs[dc % 2]; sk = ('lnsq', dc % 2)
            P.act(s_[:], R[:, dc, :], AF.Square, reads=[Rk], writes=[sk])
            P.mm(s1, self.ones_f[:], R[:, dc, :], start=(dc == 0), stop=(dc == 15), reads=['ones_f', Rk], writes=[k1], noinc=True)
            P.mm(s2, self.ones_f[:], s_[:], start=(dc == 0), stop=(dc == 15), reads=['ones_f', sk], writes=[k2])
        P.act(m[:], s1, AF.Copy, reads=[k1], writes=['ln_m'], scale=1.0 / D)
        P.act(var[:], s1, AF.Square, reads=[k1], writes=['ln_var'], scale=1.0 / D)
        P.stt(var[:], s2, 1.0 / D, var[:], ALU.mult, ALU.subtract, reads=[k2, 'ln_var'], writes=['ln_var'])
        P.act(var[:], var[:], AF.Sqrt, reads=['ln_var', 'eps'], writes=['ln_var'], bias=self.eps_ln[:], scale=1.0)
        P.op('dve', lambda e: e.reciprocal(var[:], var[:]), reads=['ln_var'], writes=['ln_var'])
        for dc in range(16):
            P.tt(R[:, dc, :], R[:, dc, :], m[:], ALU.subtract, reads=[Rk, 'ln_m'], writes=[Rk])
            P.tt(R[:, dc, :], R[:, dc, :], var[:], ALU.mult, reads=[Rk, 'ln_var'], writes=[Rk])
            P.ts(R[:, dc, :], R[:, dc, :], gcol[:, dc:dc + 1], bcol[:, dc:dc + 1], ALU.mult, ALU.add,
                 reads=[Rk, 'lng', 'lnb'], writes=[Rk])
            if Rb is not None:
                P.act(Rb[:, dc, :], R[:, dc, :], AF.Copy, reads=[Rk], writes=[Rbk])

    def p2(self, l, kind):
        P, TC = self.P, self.TC
        last = (l == self.L - 1)
        P.push()
        R = P.sb([128, 16, 512], F32)
        Rb = P.sb([128, 16, 512], BF16)
        hid = P.sb([128, 64, 512], BF16)
        hin = [P.sb([128, 512], F32) for _ in range(2)]
        sqs = [P.sb([128, 512], F32) for _ in range(2)]
        m = P.sb([128, 512], F32)
        var = P.sb([128, 512], F32)
        upw = [P.sb([128, 2, 2048], BF16) for _ in range(2)]
        dnw = [P.sb([128, 4096], BF16) for _ in range(2)]
        bout = None
        if kind == 2:
            bout = P.sb([128, 16], F32)
            P.dma(bout[:], self.bout_d, writes=['bout'])
        bk = Banks(self.ps_all, range(6))
        src_res = self.xT if l == 0 else self.hres
        res_key = 'xT' if l == 0 else 'hres'
        ku = 0
        kd = 0
        for st in range(self.NST):
            c0 = st * 512
            Rk = 'R'; Rbk = 'Rb'
            for q in range(4):
                P.dma(R[:, 4 * q:4 * q + 4, :], self.mixed[q][:, c0:c0 + 512].rearrange("(k p) t -> p k t", p=128), reads=['mixed'], writes=[Rk])
            for dc in range(16):
                hi = hin[dc % 2]; hk = ('hin', dc % 2)
                P.dma(hi[:], src_res[dc * 128:(dc + 1) * 128, c0:c0 + 512], reads=[(res_key, st)], writes=[hk])
                P.stt(R[:, dc, :], hi[:], ALPHA, R[:, dc, :], ALU.mult, ALU.add, reads=[hk, Rk], writes=[Rk])
                if bout is not None:
                    P.ts(R[:, dc, :], R[:, dc, :], bout[:, dc:dc + 1], None, ALU.add, reads=[Rk, 'bout'], writes=[Rk])
            g0 = (l * 2 + 0) * 16
            self.layernorm(R, Rk, Rb, Rbk, self.lng[:, g0:g0 + 16], self.lnb[:, g0:g0 + 16], bk, (sqs, m, var))
            for g in range(32):
                w = upw[ku % 2]; wk = ('upw', ku % 2); ku += 1
                for mm_ in range(2):
                    mc = g * 2 + mm_
                    P.dma(w[:, mm_, :], self.upb[l][mc % 8][(mc // 8) * 128:(mc // 8 + 1) * 128, :],
                          reads=[f'upb{l}'], writes=[wk])
                for mm_ in range(2):
                    mc = g * 2 + mm_
                    ps, pk = bk.next()
                    P.mmacc(ps, [(w[:, mm_, kc * 128:(kc + 1) * 128], Rb[:, kc, :]) for kc in range(16)],
                            reads=[wk, Rbk], writes=[pk])
                    rl = sqs[mc % 2]; rlk = ('lnsq', mc % 2)
                    P.act(rl[:], ps, AF.Relu, reads=[pk], writes=[rlk])
                    P.tt(hid[:, mc, :], rl[:], ps, ALU.mult, reads=[rlk, pk], writes=['hid'])
            for dc in range(16):
                ps, pk = bk.next()
                for half in range(2):
                    w = dnw[kd % 2]; wk = ('dnw', kd % 2); kd += 1
                    for qt in range(4):
                        qq = (dc % 2) * 4 + qt
                        rr = dc // 2
                        P.dma(w[qt * 32:(qt + 1) * 32, :], self.dnb[l][qq][rr * 32:(rr + 1) * 32, half * 4096:(half + 1) * 4096],
                              reads=[f'dnb{l}'], writes=[wk])
                    for hc in range(32):
                        i = half * 32 + hc
                        P.mm(ps, w[:, hc * 128:(hc + 1) * 128], hid[:, i, :], start=(i == 0), stop=(i == 63),
                             reads=[wk, 'hid'], writes=[pk], noinc=(i < 63))
                P.stt(R[:, dc, :], R[:, dc, :], ALPHA, ps, ALU.mult, ALU.add, reads=[Rk, pk], writes=[Rk])
            g1 = (l * 2 + 1) * 16
            self.layernorm(R, Rk, None if last else Rb, Rbk, self.lng[:, g1:g1 + 16], self.lnb[:, g1:g1 + 16], bk, (sqs, m, var))
            if last:
                P.dma(self.outT[:, c0:c0 + 512].rearrange("(k p) t -> p k t", p=128), R[:], reads=[Rk], writes=[('outT', st)])
            else:
                P.dma(self.hres[:, c0:c0 + 512].rearrange("(k p) t -> p k t", p=128), R[:], reads=[Rk], writes=[('hres', st)])
                P.dma(self.hT_loc[:, c0:c0 + 512].rearrange("(k p) t -> p k t", p=128), Rb[:], reads=[Rbk], writes=[('hT_loc', st)])
        P.pop()


def rope_tables(S):
    half = 32
    inv = (10000.0 ** (-np.arange(half, dtype=np.float32) / half)).astype(np.float32)
    ang = np.arange(S, dtype=np.float32)[:, None] * inv[None, :]
    cos = np.cos(ang).astype(np.float32).T
    sin = np.sin(ang).astype(np.float32).T
    cosT = np.concatenate([cos, cos], 0)
    sinT = np.concatenate([-sin, sin], 0)
    return np.ascontiguousarray(cosT), np.ascontiguousarray(sinT)


def swa_tables():
    k = np.arange(128)[:, None]
    q = np.arange(128)[None, :]
    dist = np.stack([q + 128 - k, q - k], 1)
    msk = ((dist >= 0) & (dist < 128)).astype(np.float32)
    n = np.maximum(dist, 0)
    nf = np.maximum(n, 1).astype(np.float32)
    large = 16 + (np.log(nf / np.float32(16)) / np.float32(math.log(128 / 16)) * np.float32(16)).astype(np.int32)
    large = np.minimum(large, 31)
    idx = np.where(n < 16, n, large)
    return idx, msk


def rwkv_consts():
    j = np.arange(128)[:, None]
    t = np.arange(128)[None, :]
    strict = (j < t).astype(np.float32)
    incl = (j <= t).astype(np.float32)
    blk = ((np.arange(128)[:, None] // 64) == (np.arange(128)[None, :] // 64)).astype(np.float32)
    ident = np.eye(128, dtype=np.float32)
    rst = np.ones((128, 512), np.float32)
    rst[:, 0::128] = 0.0
    return {"rw_blk": blk, "rw_id": ident,
            "rw_mm": np.ascontiguousarray(np.concatenate([strict, incl, strict, incl], 1)),
            "rw_mt4": np.ascontiguousarray(np.concatenate([strict.T] * 4, 1)),
            "rw_i4": np.ascontiguousarray(np.concatenate([ident] * 4, 1)),
            "rw_rst": rst}

def prep_inputs(inp, S, kinds):
    L = len(kinds)
    TC = S // 4
    f = lambda a: np.ascontiguousarray(np.asarray(a, dtype=np.float32))
    x = f(inp["x"])
    lng = f(inp["ln_g"])[:L]
    lnb = f(inp["ln_b"])[:L]
    lng_c = np.ascontiguousarray(lng.reshape(L * 2 * 16, 128).T)
    lnb_c = np.ascontiguousarray(lnb.reshape(L * 2 * 16, 128).T)
    cosT, sinT = rope_tables(S)
    tri = (np.arange(128)[None, :] >= np.arange(128)[:, None]).astype(np.float32)
    up_l = [lay(f(inp["mlp_up"][l])).reshape(8192, 2048) for l in range(L)]
    dn_l = [lay(f(inp["mlp_down"][l])).reshape(2048, 8192) for l in range(L)]
    maps = []
    for c in range(8):
        b, j = c // 4, c % 4
        m = {}
        m["xT"] = np.ascontiguousarray(x[b, j * TC:(j + 1) * TC, :].T)
        m["lng"] = lng_c
        m["lnb"] = lnb_c
        for l in range(L):
            m[f"up{l}"] = np.ascontiguousarray(up_l[l][c * 1024:(c + 1) * 1024])
            m[f"dn{l}"] = np.ascontiguousarray(dn_l[l][c * 256:(c + 1) * 256])
        cnt = [0, 0, 0]
        for l, kind in enumerate(kinds):
            i = cnt[kind]
            cnt[kind] += 1
            pre = f"L{l}_"
            if kind == 0:
                m["cosT"] = cosT; m["sinT"] = sinT; m["tri"] = tri
                w_in = f(inp["mla_w_in"][i])
                rope_c = w_in[:, 1024:1088]
                sw = np.concatenate([rope_c[:, 32:], rope_c[:, :32]], 1)
                m[pre + "win"] = lay(np.concatenate([w_in[:, :1024], rope_c, sw], 1))
                wq = f(inp["mla_w_q_b"][i]); wkv = f(inp["mla_w_kv_b"][i])
                cols = []
                kcols = []
                vcols = []
                for hh in range(4 * j, 4 * j + 4):
                    qh = wq[:, hh * 192:(hh + 1) * 192]
                    r_ = qh[:, 128:]
                    cols += [qh[:, :128], r_, np.concatenate([r_[:, 32:], r_[:, :32]], 1)]
                    kvh = wkv[:, hh * 256:(hh + 1) * 256]
                    kcols.append(kvh[:, :128]); vcols.append(kvh[:, 128:])
                m[pre + "wqb"] = lay(np.concatenate(cols, 1))
                m[pre + "wkn"] = lay(np.concatenate(kcols, 1))
                m[pre + "wv"] = lay_rhs(np.concatenate(vcols, 1))
                m[pre + "wout"] = lay(f(inp["mla_w_out"][i])[j * 512:(j + 1) * 512, :])
                m[pre + "gq"] = colvec(f(inp["mla_q_norm"][i]))
                m[pre + "gkv"] = colvec(f(inp["mla_kv_norm"][i]))
            if kind == 2:
                w_in = f(inp["swa_w_in"][i]); b_in = f(inp["swa_b_in"][i])
                qc = slice(j * 512, (j + 1) * 512)
                kc_ = slice(2048 + j * 64, 2048 + (j + 1) * 64)
                vc_ = slice(2304 + j * 64, 2304 + (j + 1) * 64)
                m[pre + "wq"] = lay(w_in[:, qc])
                m[pre + "wkv"] = lay(np.concatenate([w_in[:, kc_], w_in[:, vc_]], 1))
                m[pre + "wvr"] = lay_rhs(w_in[:, vc_])
                m[pre + "bq"] = np.ascontiguousarray(b_in[qc].reshape(8, 64).T)
                m[pre + "bk"] = np.ascontiguousarray(b_in[kc_].reshape(64, 1))
                m[pre + "bv"] = np.ascontiguousarray(np.broadcast_to(b_in[vc_][None, :], (128, 64)))
                wo = f(inp["swa_w_out"][i])[j * 512:(j + 1) * 512, :]
                m[pre + "wout"] = lay(wo)
                m[pre + "bout"] = colvec(f(inp["swa_b_out"][i]))
                idx, msk = swa_tables()
                rb = f(inp["rel_bias"])[:, j * 8:(j + 1) * 8]
                bg = rb[idx]
                m[pre + "biasg"] = np.ascontiguousarray(bg.transpose(0, 1, 3, 2)).reshape(128, 2048)
                m[pre + "wmask"] = np.ascontiguousarray(np.broadcast_to(msk[:, :, None, :], (128, 2, 8, 128))).reshape(128, 2048)
                m[pre + "sinks"] = np.ascontiguousarray(np.broadcast_to(f(inp["swa_sinks"][i])[None, j * 8:(j + 1) * 8], (64, 8)))
            if kind == 1:
                my = slice(j * 512, (j + 1) * 512)
                w_in = f(inp["rwkv_w_in"][i])
                m[pre + "wr"] = lay(w_in[0][:, my]); m[pre + "wk"] = lay(w_in[1][:, my]); m[pre + "wv"] = lay(w_in[2][:, my])
                pad_c = lambda W: np.pad(W, ((0, 0), (0, 128 - W.shape[1])))
                pad_r = lambda W: np.pad(W, ((0, 128 - W.shape[0]), (0, 0)))
                m[pre + "w1"] = lay(pad_c(f(inp["rwkv_w1"][i]))); m[pre + "w2"] = lay(pad_r(f(inp["rwkv_w2"][i])[:, my]))
                m[pre + "a1"] = lay(pad_c(f(inp["rwkv_a1"][i]))); m[pre + "a2"] = lay(pad_r(f(inp["rwkv_a2"][i])[:, my]))
                m[pre + "g1"] = lay(f(inp["rwkv_g1"][i])); m[pre + "g2"] = lay(f(inp["rwkv_g2"][i])[:, my])
                m[pre + "wout"] = lay(f(inp["rwkv_w_out"][i])[my, :])
                m[pre + "mix"] = np.ascontiguousarray(np.concatenate([colvec(f(inp["rwkv_mix"][i])[v]) for v in range(6)], 1))
                rkf = f(inp["rwkv_r_k"][i]).reshape(-1)
                vs = [f(inp["rwkv_w0"][i]), f(inp["rwkv_a0"][i]), f(inp["rwkv_k_k"][i]), f(inp["rwkv_k_a"][i]), rkf,
                      f(inp["rwkv_ln_g"][i]), f(inp["rwkv_ln_b"][i])]
                m[pre + "cols"] = np.ascontiguousarray(np.concatenate([colvec(v[my]) for v in vs], 1))
                m.update(rwkv_consts())
        maps.append(m)
    return maps


_CACHE = {}


def run(inp, S, kinds):
    key = (S, tuple(kinds))
    if key not in _CACHE:
        b = Builder(S, list(kinds))
        nc = b.build()
        _CACHE[key] = (nc, b)
    nc, b = _CACHE[key]
    maps = prep_inputs(inp, S, kinds)
    maps = [{k: v for k, v in m.items() if k in b.ext_shapes} for m in maps]
    for m in maps:
        assert set(m.keys()) == set(b.ext_shapes.keys()), (sorted(set(m.keys()) ^ set(b.ext_shapes.keys())))
        for k, v in m.items():
            assert tuple(v.shape) == b.ext_shapes[k], (k, v.shape, b.ext_shapes[k])
    res = run_bass_kernel_spmd(nc, maps, core_ids=list(range(8)))
    TC = S // 4
    out = np.zeros((2, S, D), np.float32)
    for c in range(8):
        bb, j = c // 4, c % 4
        out[bb, j * TC:(j + 1) * TC, :] = res.results[c]["outT"].T
    return out


def kernel(**inputs):
    return run(inputs, 8192, [0, 1, 2, 0])
```
